# Optimizing a Trainium2 kernel written in Bass

```python
import jax, jax.numpy as jnp
from jax import lax
import numpy as np

D_MODEL = 1024
BATCH = 16
SEQ = 2048
DEPTH = 1

CTX_LEN = 256
GRID_W = 64
D_FOURIER = 512
N_FOURIER_GROUPS = 4
FOURIER_GROUP = D_FOURIER // N_FOURIER_GROUPS
D_RET = 512
N_RET_HEADS = 4
RET_HEAD_DIM = D_RET // N_RET_HEADS
D_MIX = D_FOURIER + D_RET
IN_SPLITS = (D_FOURIER, 2 * D_FOURIER, 2 * D_FOURIER + D_RET,
             2 * D_FOURIER + 2 * D_RET, 2 * D_FOURIER + 3 * D_RET)
D_IN = 2 * D_FOURIER + 4 * D_RET
RET_CHUNK = 128
ROPE_BASE = 10000.0
QK_SCALE = RET_HEAD_DIM ** -0.5
EPS = 1e-6

kernel_name = "hymba_fnet_retnet_prefix_dit_block"


def rms_norm(x, gain):
    xf = x.astype(jnp.float32)
    y = xf * lax.rsqrt(jnp.mean(xf * xf, axis=-1, keepdims=True) + EPS)
    return (y * gain.astype(jnp.float32)).astype(x.dtype)


def adaln_params(cvec, w_ada, b_ada):
    m = jax.nn.silu(cvec) @ w_ada + b_ada
    shift, scale, gate = jnp.split(m, 3, axis=-1)
    return shift[:, None, :], scale[:, None, :], gate[:, None, :]


def modulate(h, shift, scale):
    return h * (1.0 + scale) + shift


def split_heads(t):
    return t.reshape(t.shape[0], t.shape[1], N_RET_HEADS, RET_HEAD_DIM)


def axial_rotary(t, row, col):
    half = RET_HEAD_DIM // 2
    n_freq = half // 2
    inv_freq = ROPE_BASE ** (-jnp.arange(n_freq, dtype=jnp.float32) / n_freq)

    def rot(u, pos):
        ang = pos.astype(jnp.float32)[:, None] * inv_freq[None, :]
        cos = jnp.cos(ang)[None, :, None, :]
        sin = jnp.sin(ang)[None, :, None, :]
        u1, u2 = u[..., :n_freq].astype(jnp.float32), u[..., n_freq:].astype(jnp.float32)
        return jnp.concatenate([u1 * cos - u2 * sin, u1 * sin + u2 * cos], axis=-1)

    out = jnp.concatenate([rot(t[..., :half], row), rot(t[..., half:], col)], axis=-1)
    return out.astype(t.dtype)


def retention_scan(q, k, v, log_gamma, init_state, inclusive):
    bsz, length, heads, dh = q.shape
    n_chunks = length // RET_CHUNK

    def to_chunks(t):
        return t.reshape(bsz, n_chunks, RET_CHUNK, heads, dh).transpose(1, 0, 3, 2, 4)

    idx = jnp.arange(RET_CHUNK, dtype=jnp.float32)
    diff = idx[:, None] - idx[None, :]
    mask = (diff >= 0) if inclusive else (diff > 0)
    decay_mat = jnp.where(mask[None], jnp.exp(log_gamma[:, None, None] * jnp.where(mask, diff, 0.0)[None]), 0.0)
    q_decay = jnp.exp(log_gamma[:, None] * (idx[None, :] + 1.0))
    k_decay = jnp.exp(log_gamma[:, None] * (RET_CHUNK - 1.0 - idx[None, :]))
    chunk_decay = jnp.exp(log_gamma * RET_CHUNK)

    def step(state, qkv):
        qc, kc, vc = [t.astype(jnp.float32) for t in qkv]
        scores = jnp.einsum('bhnd,bhmd->bhnm', qc, kc) * decay_mat[None]
        inner = jnp.einsum('bhnm,bhme->bhne', scores, vc)
        cross = jnp.einsum('bhnd,bhde->bhne', qc * q_decay[None, :, :, None], state)
        new_state = chunk_decay[None, :, None, None] * state + jnp.einsum(
            'bhmd,bhme->bhde', kc * k_decay[None, :, :, None], vc)
        return new_state, inner + cross

    final_state, out = lax.scan(step, init_state, (to_chunks(q), to_chunks(k), to_chunks(v)))
    out = out.transpose(1, 0, 3, 2, 4).reshape(bsz, length, heads, dh)
    return out, final_state


def bidirectional_retention(q, k, v, decay_logit, init_fwd, init_bwd):
    log_g = jax.nn.log_sigmoid(decay_logit.astype(jnp.float32))
    out_f, fin_f = retention_scan(q, k, v, log_g[0], init_fwd, inclusive=True)
    out_b, fin_b = retention_scan(q[:, ::-1], k[:, ::-1], v[:, ::-1], log_g[1], init_bwd, inclusive=False)
    return out_f + out_b[:, ::-1], fin_f, fin_b


def gated_retention(o, r_gate, gn_gain):
    mu = jnp.mean(o, axis=-1, keepdims=True)
    var = jnp.mean(jnp.square(o - mu), axis=-1, keepdims=True)
    o = ((o - mu) * lax.rsqrt(var + EPS)).reshape(o.shape[0], o.shape[1], D_RET)
    o = o * gn_gain.astype(jnp.float32) * jax.nn.silu(r_gate.astype(jnp.float32))
    return o.astype(r_gate.dtype)


def fourier_branch(f_in, f_gate, w_fourier):
    bsz, length, _ = f_in.shape
    u = f_in.reshape(bsz, length, N_FOURIER_GROUPS, FOURIER_GROUP).astype(jnp.float32)
    mixed = jnp.real(jnp.fft.fft2(u, axes=(1, 3), norm='ortho')).astype(f_in.dtype)
    y = jnp.einsum('blgc,gcd->blgd', mixed, w_fourier).reshape(bsz, length, D_FOURIER)
    return y * jax.nn.silu(f_gate)


def setup_inputs(seed: int = 0) -> dict:
    key = jax.random.key(seed)
    ks = jax.random.split(key, 16)
    n = jax.random.normal
    base_gamma = 1.0 - 2.0 ** (-5.0 - np.arange(N_RET_HEADS, dtype=np.float32))
    base_logit = jnp.asarray(np.log(base_gamma / (1.0 - base_gamma)), jnp.float32)
    return {
        "x": n(ks[0], (BATCH, SEQ, D_MODEL), jnp.float32),
        "c": n(ks[1], (BATCH, D_MODEL), jnp.float32),
        "ctx": n(ks[2], (BATCH, CTX_LEN, D_MODEL), jnp.float32),
        "c_ctx": n(ks[3], (D_MODEL,), jnp.float32),
        "w_ada": n(ks[4], (DEPTH, D_MODEL, 3 * D_MODEL), jnp.float32) * (0.5 * D_MODEL ** -0.5),
        "b_ada": n(ks[5], (DEPTH, 3 * D_MODEL), jnp.float32) * 0.01,
        "g_pre": 1.0 + 0.02 * n(ks[6], (DEPTH, D_MODEL), jnp.float32),
        "g_post": 1.0 + 0.02 * n(ks[7], (DEPTH, D_MODEL), jnp.float32),
        "w_in": n(ks[8], (DEPTH, D_MODEL, D_IN), jnp.float32) * D_MODEL ** -0.5,
        "w_fourier": n(ks[9], (DEPTH, N_FOURIER_GROUPS, FOURIER_GROUP, FOURIER_GROUP), jnp.float32) * FOURIER_GROUP ** -0.5,
        "decay_logit": base_logit[None, None, :] + 0.05 * n(ks[10], (DEPTH, 2, N_RET_HEADS), jnp.float32),
        "ret_gn_gain": 1.0 + 0.02 * n(ks[11], (DEPTH, D_RET), jnp.float32),
        "w_out": n(ks[12], (DEPTH, D_MIX, D_MODEL), jnp.float32) * D_MIX ** -0.5,
    }


def reference(x, c, ctx, c_ctx, w_ada, b_ada, g_pre, g_post, w_in, w_fourier, decay_logit, ret_gn_gain, w_out):
    bsz, seq_len, _ = x.shape
    rows = seq_len // GRID_W
    row = jnp.repeat(jnp.arange(rows, dtype=jnp.int32), GRID_W)
    col = jnp.tile(jnp.arange(GRID_W, dtype=jnp.int32), rows)
    zero_state = jnp.zeros((bsz, N_RET_HEADS, RET_HEAD_DIM, RET_HEAD_DIM), jnp.float32)

    for layer in range(DEPTH):
        shift, scale, gate = adaln_params(c, w_ada[layer], b_ada[layer])
        shift_c, scale_c, gate_c = adaln_params(c_ctx[None, :], w_ada[layer], b_ada[layer])

        h_c = modulate(rms_norm(ctx, g_pre[layer]), shift_c, scale_c)
        fc_in, fc_gate, qc, kc, vc, rc_gate = jnp.split(h_c @ w_in[layer], IN_SPLITS, axis=-1)
        ret_c, st_fwd, st_bwd = bidirectional_retention(
            split_heads(qc) * QK_SCALE, split_heads(kc), split_heads(vc),
            decay_logit[layer], zero_state, zero_state)

        h = modulate(rms_norm(x, g_pre[layer]), shift, scale)
        f_in, f_gate, q, k, v, r_gate = jnp.split(h @ w_in[layer], IN_SPLITS, axis=-1)
        q = axial_rotary(split_heads(q), row, col) * QK_SCALE
        k = axial_rotary(split_heads(k), row, col)
        ret, _, _ = bidirectional_retention(q, k, split_heads(v), decay_logit[layer], st_fwd, st_bwd)
        y = jnp.concatenate([fourier_branch(f_in, f_gate, w_fourier[layer]),
                             gated_retention(ret, r_gate, ret_gn_gain[layer])], axis=-1) @ w_out[layer]
        x_next = x + gate * rms_norm(y, g_post[layer])

        if layer < DEPTH - 1:
            y_c = jnp.concatenate([fourier_branch(fc_in, fc_gate, w_fourier[layer]),
                                   gated_retention(ret_c, rc_gate, ret_gn_gain[layer])], axis=-1) @ w_out[layer]
            ctx = ctx + gate_c * rms_norm(y_c, g_post[layer])
        x = x_next

    return x
```

```python
import numpy as np
from contextlib import ExitStack
import concourse.bass as bass
import concourse.mybir as mybir
from concourse.bass_utils import run_bass_kernel_spmd

F32 = mybir.dt.float32
BF16 = mybir.dt.bfloat16
I32 = mybir.dt.int32
ALU = mybir.AluOpType
ACTF = mybir.ActivationFunctionType

N_CORES = 8
NB = 2
L = 2048
D = 1024
LC = 256
NT = L // 128
QK_SCALE = 128.0 ** -0.5
EPS = 1e-6
ENGS = ("pe", "act", "dve", "pool", "sp")
BLK = 128
SAME_ENGINE_WAR = True


def _esize(dt):
    if dt == BF16:
        return 2
    return 4


def ap_range(ap):
    t = ap.tensor
    es = _esize(ap.dtype)
    row = 1
    for s in list(t.shape)[1:]:
        row *= int(s)
    f0 = int(ap.offset) % row
    span = 0
    for (st, cnt) in list(ap.ap)[1:]:
        span += (int(cnt) - 1) * abs(int(st))
    return (t.name, f0 * es, (f0 + span + 1) * es)


class Sched:
    def __init__(self, nc):
        self.nc = nc
        self.ops = []
        self.eng_ops = {e: [] for e in ENGS}
        self.waited = {e: {} for e in ENGS}
        self.blocks = {}
        self.dma_cnt = {}
        self.bank = {}
        self.limit = 10 ** 9
        self.nrec = 0
        self.log = []

    def _blocks(self, ap):
        name, lo, hi = ap_range(ap)
        return [(name, b) for b in range(lo // BLK, (hi - 1) // BLK + 1)]

    def _collect(self, reads, writes):
        raw, other = set(), set()
        rb, wb = [], []
        for ap in reads:
            for k in self._blocks(ap):
                rb.append(k)
                ent = self.blocks.get(k)
                if ent and ent[0] is not None:
                    raw.add(ent[0])
        for ap in writes:
            for k in self._blocks(ap):
                wb.append(k)
                ent = self.blocks.get(k)
                if ent:
                    if ent[0] is not None:
                        other.add(ent[0])
                    for r in ent[1]:
                        other.add(r)
        return raw, other, rb, wb

    def _commit(self, tok, rb, wb):
        for k in rb:
            ent = self.blocks.setdefault(k, [None, []])
            ent[1].append(tok)
        for k in wb:
            self.blocks[k] = [tok, []]

    def _deps_to_waits(self, engine, raw, other):
        waits = []
        best = {}
        for tok in raw | other:
            kind, key, val = tok
            if kind == "e" and key == engine:
                if engine in ("pe", "sp"):
                    continue
                if tok not in raw and not SAME_ENGINE_WAR:
                    continue
            k = (kind, key)
            if val > best.get(k, -1):
                best[k] = val
        for k, val in best.items():
            if self.waited[engine].get(k, -1) >= val:
                continue
            self.waited[engine][k] = val
            waits.append((k[0], k[1], val))
        return waits

    def op(self, engine, fn, reads=(), writes=(), force=False):
        self.nrec += 1
        if self.nrec > self.limit and not force:
            return None
        import sys as _s
        self.log.append((self.nrec, engine, [_s._getframe(i).f_lineno for i in (1, 2)]))
        raw, other, rb, wb = self._collect(reads, writes)
        banks = set()
        for ap in list(reads) + list(writes):
            nm = ap.tensor.name
            if nm.startswith("ps"):
                banks.add(nm)
        for bk in banks:
            for eng2, tok in self.bank.get(bk, {}).items():
                if eng2 != engine:
                    other.add(tok)
        waits = self._deps_to_waits(engine, raw, other)
        idx = len(self.eng_ops[engine])
        o = dict(kind="op", engine=engine, fn=fn, waits=waits, idx=idx, signal=False)
        self.eng_ops[engine].append(o)
        self.ops.append(o)
        self._commit(("e", engine, idx), rb, wb)
        for bk in banks:
            self.bank.setdefault(bk, {})[engine] = ("e", engine, idx)
        return o

    def dma(self, queue, semkey, pairs, reads=(), writes=(), force=False, deps=(), kw=None):
        self.nrec += 1
        if self.nrec > self.limit and not force:
            return None
        import sys as _s
        self.log.append((self.nrec, queue, [_s._getframe(i).f_lineno for i in (1, 2)]))
        raw, other, rb, wb = self._collect(reads, writes)
        for d in deps:
            other.add(d)
        waits = self._deps_to_waits(queue, raw, other)
        val = self.dma_cnt.get(semkey, 0) + 16 * len(pairs)
        self.dma_cnt[semkey] = val
        idx = len(self.eng_ops[queue])
        o = dict(kind="dma", engine=queue, semkey=semkey, pairs=pairs, waits=waits, idx=idx, kw=kw or {}, tok=("d", semkey, val))
        self.eng_ops[queue].append(o)
        self.ops.append(o)
        self._commit(("d", semkey, val), rb, wb)
        return o

    def emit(self, stack):
        nc = self.nc
        for o in self.ops:
            for (kind, key, val) in o["waits"]:
                if kind == "e":
                    self.eng_ops[key][val]["signal"] = True
        sems = {}
        for e in ENGS:
            sems[("e", e)] = stack.enter_context(nc.semaphore("s_" + e))
        for k in self.dma_cnt:
            sems[("d", k)] = stack.enter_context(nc.semaphore("d_" + str(k)))
        for e in ENGS:
            c = 0
            for o in self.eng_ops[e]:
                if o["kind"] == "op" and o["signal"]:
                    c += 1
                    o["count"] = c
        final_dma = dict(self.dma_cnt)

        def run(engine_name, eng):
            for o in self.eng_ops[engine_name]:
                for (kind, key, val) in o["waits"]:
                    v = self.eng_ops[key][val]["count"] if kind == "e" else val
                    eng.wait_ge(sems[(kind, key)], v)
                if o["kind"] == "op":
                    ins = o["fn"](eng)
                    if o["signal"]:
                        ins.then_inc(sems[("e", engine_name)], 1)
                else:
                    for (out_ap, in_ap) in o["pairs"]:
                        eng.dma_start(out=out_ap, in_=in_ap, **o["kw"]).then_inc(sems[("d", o["semkey"])], 16)
            if engine_name == "sp":
                for k, v in final_dma.items():
                    eng.wait_ge(sems[("d", k)], v)

        with nc.Block() as block:
            @block.tensor
            def _(eng):
                run("pe", eng)

            @block.scalar
            def _(eng):
                run("act", eng)

            @block.vector
            def _(eng):
                run("dve", eng)

            @block.gpsimd
            def _(eng):
                run("pool", eng)

            @block.sync
            def _(eng):
                run("sp", eng)


def host_consts():
    c = {}
    c["c_ident"] = np.eye(128, dtype=np.float32)
    j = np.arange(128)[:, None, None]
    r = np.arange(16)[None, :, None]
    jp = np.arange(128)[None, None, :]
    ang = 2 * np.pi * ((jp * (16 * j + r)) % 2048) / 2048.0
    c["c_R1"] = np.concatenate([np.cos(ang), np.sin(ang)], axis=2).astype(np.float32).reshape(128, 16 * 256)
    rr = np.arange(16)
    a16 = 2 * np.pi * ((rr[:, None] * rr[None, :]) % 16) / 16.0
    C16 = np.kron(np.eye(8), np.cos(a16))
    S16 = np.kron(np.eye(8), np.sin(a16))
    c["c_M"] = np.concatenate([C16, -S16, -S16, -C16], axis=1).astype(np.float32)
    cc = np.arange(128)
    a128 = 2 * np.pi * ((cc[:, None] * cc[None, :]) % 128) / 128.0
    c["c_CS"] = (np.concatenate([np.cos(a128), np.sin(a128)], axis=1) / 512.0).astype(np.float32)
    n = np.arange(L)
    row = (n // 64).astype(np.float32)
    col = (n % 64).astype(np.float32)
    inv = (10000.0 ** (-np.arange(32, dtype=np.float32) / 32)).astype(np.float32)
    ar = (row[:, None] * inv[None, :]).astype(np.float32)
    ac = (col[:, None] * inv[None, :]).astype(np.float32)
    cosr, sinr, cosc, sinc = np.cos(ar), np.sin(ar), np.cos(ac), np.sin(ac)
    COS = np.concatenate([cosr, cosr, cosc, cosc], axis=1)
    SIN = np.concatenate([-sinr, sinr, -sinc, sinc], axis=1)
    c["c_cos"] = COS.reshape(NT, 128, 128).transpose(1, 0, 2).reshape(128, NT * 128).astype(np.float32)
    c["c_sin"] = SIN.reshape(NT, 128, 128).transpose(1, 0, 2).reshape(128, NT * 128).astype(np.float32)
    m = np.arange(128)
    diff = m[None, :] - m[:, None]
    c["c_pn"] = np.concatenate([np.maximum(diff, 0), np.maximum(-diff, 0)], axis=1).astype(np.float32)
    idx = np.zeros((128, 8), np.float32)
    idx[:, 0] = 127 - m
    idx[:, 1] = m
    idx[:, 2] = m + 1
    idx[:, 3] = 128 - m
    idx[:, 4] = 255 - m
    idx[:, 5] = 128 + m
    c["c_idx"] = idx
    return c


CONST_SHAPES = {"c_ident": [128, 128], "c_R1": [128, 4096], "c_M": [128, 512], "c_CS": [128, 256],
                "c_cos": [128, 2048], "c_sin": [128, 2048], "c_pn": [128, 256], "c_idx": [128, 8]}


DBG = {}


def build_program(stages=("ctx", "A", "F", "R", "O"), nb=NB, dumps=(), lvl=99):
    nc = bass.Bass("TRN2", target_bir_lowering=False)

    def din(name, shape):
        return nc.dram_tensor(name, shape, F32, kind="ExternalInput").ap()

    x_d = din("x", [NB, L, D])
    ctx_d = din("ctx", [NB, LC, D])
    cT_d = din("cT", [128, 24])
    wada_d = din("w_ada", [D, 3 * D])
    bT_d = din("bT", [128, 24])
    bgate_d = din("b_gate", [D])
    gpre_d = din("g_preT", [128, 8])
    gpost_d = din("g_post", [D])
    win_d = din("w_in", [D, 3 * D])
    wf_d = din("w_f", [4, 128, 128])
    dl_d = din("dlog", [8])
    gn_d = din("gnT", [128, 4])
    wout_d = din("w_out", [D, D])
    cd = {k: din(k, v) for k, v in CONST_SHAPES.items()}
    out_d = nc.dram_tensor("out", [NB, L, D], F32, kind="ExternalOutput").ap()
    gg_scr = nc.dram_tensor("gg_scr", [NB, 128, D], F32).ap()

    st = ExitStack()
    S = Sched(nc)
    import os
    S.limit = int(os.environ.get("KLIMIT", "1000000000"))
    TOT = 205 * 1024
    R = st.enter_context(nc.sbuf_tensor("R", [128, TOT // 2], BF16))
    R32 = R.bitcast(F32)
    RI = R.bitcast(I32)
    ps = [st.enter_context(nc.psum_tensor("ps%d" % i, [128, 512], F32)) for i in range(8)]
    psb = [p.bitcast(BF16) for p in ps]

    class Arena:
        def __init__(self, base):
            self.off = base

        def take(self, nbytes):
            o = self.off
            self.off += (nbytes + 63) // 64 * 64
            assert self.off <= TOT, ("SBUF overflow", self.off)
            return o

    def v16(o, n):
        return R[:, o // 2: o // 2 + n]

    def v32(o, n):
        return R32[:, o // 4: o // 4 + n]

    def vi(o, n):
        return RI[:, o // 4: o // 4 + n]

    A0 = Arena(0)
    o_WINF = A0.take(8 * 1024 * 2)
    o_WINR = A0.take(8 * 2048 * 2)
    o_BIG = A0.take(65536)
    o_ident = A0.take(256)
    o_M = A0.take(1024)
    o_CS = A0.take(512)
    o_wf = A0.take(1024)
    o_AB = A0.take(2048)
    o_DT = A0.take(1024)
    o_QM = A0.take(3072)
    o_KT = A0.take(4 * 1024)
    o_COS = A0.take(4096)
    o_SIN = A0.take(4096)
    o_identf = A0.take(512)
    o_GI = A0.take(8 * 128 * 2)
    o_pn = A0.take(1024)
    o_idx = A0.take(32)
    o_dl = A0.take(32 * 6)
    o_kq = A0.take(24 * 4)
    o_cT = A0.take(96)
    o_siluT = A0.take(48)
    o_bT = A0.take(96)
    o_gpre = A0.take(32)
    o_gn = A0.take(16)
    o_adaT = A0.take(192)
    o_aT = A0.take(96)
    o_shT = A0.take(96)
    o_S0 = A0.take(2 * 4096)
    P0 = A0.off

    WINF = v16(o_WINF, 8 * 1024).rearrange("p (k c) -> p k c", k=8)
    WINR = v16(o_WINR, 8 * 2048).rearrange("p (k c) -> p k c", k=8)

    def W(ic, c0, c1):
        if c1 <= 1024:
            return WINF[:, ic, c0:c1]
        return WINR[:, ic, c0 - 1024:c1 - 1024]
    hT = v16(o_BIG, 16384).rearrange("p (k n) -> p k n", k=8)
    mixT = v16(o_BIG + 32768, 16384).rearrange("p (k n) -> p k n", k=8)
    WADA = v16(o_BIG, 8 * 2048).rearrange("p (k c) -> p k c", k=8)
    WADG = v16(P0, 8 * 1024).rearrange("p (k c) -> p k c", k=8)
    ident = v16(o_ident, 128)
    Mbf = v16(o_M, 512)
    CSbf = v16(o_CS, 256)
    wfbf = v16(o_wf, 512).rearrange("p (g d) -> p g d", g=4)
    ABm = v16(o_AB, 1024).rearrange("p (g s d) -> p g s d", g=4, s=2)
    DT = v16(o_DT, 512).rearrange("p (h n) -> p h n", h=4)
    QM = v16(o_QM, 1536).rearrange("p (h n) -> p h n", h=4)
    KT = v16(o_KT, 2048).rearrange("p (t h d) -> p t h d", t=4, h=4)
    COSt = v16(o_COS, 2048).rearrange("p (t d) -> p t d", t=NT)
    SINt = v16(o_SIN, 2048).rearrange("p (t d) -> p t d", t=NT)
    identf = v32(o_identf, 128)
    GI = v16(o_GI, 1024).rearrange("p (k n) -> p k n", k=8)
    pn = v32(o_pn, 256)
    idx = v32(o_idx, 8)
    dl = v32(o_dl, 8)
    e1 = v32(o_dl + 32, 8)
    spl = v32(o_dl + 64, 8)
    lg = v32(o_dl + 96, 8)
    gC = v32(o_dl + 128, 8)
    kq = v32(o_kq, 24)
    cTs = v32(o_cT, 24)
    siluT = v16(o_siluT, 24)
    bT = v32(o_bT, 24)
    gpre = v32(o_gpre, 8)
    gn = v32(o_gn, 4)
    adaT = v32(o_adaT, 48)
    aT = v32(o_aT, 24).rearrange("p (k r) -> p k r", k=8)
    shT = v32(o_shT, 24).rearrange("p (k r) -> p k r", k=8)
    GGTOK = {}
    S0b = [v32(o_S0 + b * 4096, 1024).rearrange("p (s h e) -> p s h e", s=2, h=4) for b in range(NB)]

    def mm(items):
        reads = [it[1] for it in items] + [it[2] for it in items]
        writes = [it[0] for it in items]

        def fn(e, items=items):
            ins = None
            for (o, l, r, s0, s1) in items:
                ins = e.matmul(o, l, r, start=s0, stop=s1)
            return ins
        S.op("pe", fn, reads, writes)

    def tr(items):
        reads = [it[1] for it in items] + [ident]
        writes = [it[0] for it in items]

        def fn(e, items=items):
            ins = None
            for (o, i) in items:
                ins = e.transpose(out=o, in_=i, identity=ident)
            return ins
        S.op("pe", fn, reads, writes)

    def act(out, in_, func, reads=None, **kw):
        extra = [v for v in kw.values() if not isinstance(v, (int, float))]
        S.op("act", lambda e: e.activation(out=out, in_=in_, func=func, **kw),
             [in_] + extra + (reads or []), [out] + ([kw["accum_out"]] if "accum_out" in kw else []))

    def tt(eng, out, in0, in1, op):
        S.op(eng, lambda e: e.tensor_tensor(out=out, in0=in0, in1=in1, op=op), [in0, in1], [out])

    def ts(eng, out, in0, s1, s2, op0, op1=None):
        rd = [in0] + [s for s in (s1, s2) if s is not None and not isinstance(s, (int, float))]
        if op1 is None:
            S.op(eng, lambda e: e.tensor_scalar(out=out, in0=in0, scalar1=s1, scalar2=None, op0=op0), rd, [out])
        else:
            S.op(eng, lambda e: e.tensor_scalar(out=out, in0=in0, scalar1=s1, scalar2=s2, op0=op0, op1=op1), rd, [out])

    def stt(eng, out, in0, sc, in1, op0, op1):
        rd = [in0, in1] + ([sc] if not isinstance(sc, (int, float)) else [])
        S.op(eng, lambda e: e.scalar_tensor_tensor(out=out, in0=in0, scalar=sc, in1=in1, op0=op0, op1=op1), rd, [out])

    def cp(eng, out, in_):
        if eng == "act":
            S.op("act", lambda e: e.activation(out=out, in_=in_, func=ACTF.Copy), [in_], [out])
        else:
            S.op(eng, lambda e: e.tensor_copy(out=out, in_=in_), [in_], [out])

    def rsqrt_chain(v_off, out, tA_off, tB_off, n):
        v = v32(v_off, n)
        tA, tB = v32(tA_off, n), v32(tB_off, n)
        tAi, vin = vi(tA_off, n), vi(v_off, n)
        S.op("dve", lambda e: e.tensor_scalar(out=tAi, in0=vin, scalar1=1, scalar2=None,
                                              op0=ALU.arith_shift_right), [v], [tA])
        S.op("dve", lambda e: e.tensor_scalar(out=tAi, in0=tAi, scalar1=-1, scalar2=0x5F1FFFF9,
                                              op0=ALU.mult, op1=ALU.add), [tA], [tA])
        C1 = 0.703952253
        C2 = 2.38924456
        steps = [(-C1, C1 * C2), (-0.5, 1.5)]
        for it, (ca, cb) in enumerate(steps):
            tt("dve", tB, v, tA, ALU.mult)
            stt("dve", tB, tB, ca, tA, ALU.mult, ALU.mult)
            stt("dve", out if it == len(steps) - 1 else tA, tB, cb, tA, ALU.add, ALU.mult)

    S.dma("pool", "c0", [(ident, cd["c_ident"]), (Mbf, cd["c_M"]), (CSbf, cd["c_CS"]),
                         (COSt, cd["c_cos"].rearrange("p (t d) -> p t d", t=NT)),
                         (SINt, cd["c_sin"].rearrange("p (t d) -> p t d", t=NT)),
                         (wfbf, wf_d.rearrange("g c d -> c g d"))],
          writes=[ident, Mbf, CSbf, COSt, SINt, wfbf])
    S.dma("sp", "c1", [(identf, cd["c_ident"]), (pn, cd["c_pn"]), (idx, cd["c_idx"]),
                       (dl, dl_d.partition_broadcast(128)), (cTs, cT_d), (bT, bT_d),
                       (gpre, gpre_d), (gn, gn_d)],
          writes=[identf, pn, idx, dl, cTs, bT, gpre, gn])
    wada_v = wada_d.rearrange("(k p) c -> p k c", p=128)
    S.dma("pool", "wada", [(WADA[:, k, :], wada_v[:, k, 0:2048]) for k in range(8)], writes=[WADA[:, :, :]])
    win_v = win_d.rearrange("(k p) c -> p k c", p=128)

    def load_WINF():
        S.dma("pool", "winf", [(WINF[:, :, :], win_v[:, :, 0:1024])], writes=[WINF[:, :, :]])
    pieces = []
    for h in range(4):
        pieces.append((1024 + h * 128, 128, h * 384))
        pieces.append((1536 + h * 128, 128, h * 384 + 128))
        pieces.append((2048 + h * 128, 128, h * 384 + 256))
    pieces.append((2560, 512, 1536))
    LASTX = [None]

    def load_WINR():
        S.dma("pool", "win", [(WINR[:, :, d0:d0 + n], win_v[:, :, s0:s0 + n]) for (s0, n, d0) in pieces],
              writes=[WINR[:, :, :]], deps=[LASTX[0]] if LASTX[0] else [])
    load_WINF()
    bg_rep = v32(P0 + 16384, 1024)
    gp_rep = v32(P0 + 20480, 1024)

    if lvl >= 1:
        act(e1, dl, ACTF.Exp, scale=-1.0)
        act(spl, e1, ACTF.Ln, bias=1.0)
        ts("dve", lg, spl, -1.0, None, ALU.mult)
        act(gC, lg, ACTF.Exp, scale=128.0)
        for (i, (lo, ic)) in enumerate([(0, 0), (4, 1), (0, 2), (4, 3), (0, 4), (4, 5)]):
            act(kq[:, 4 * i:4 * i + 4], lg[:, lo:lo + 4], ACTF.Exp, scale=idx[:, ic:ic + 1])
        tmpD = v32(P0, 128)
        for h in range(4):
            ts("dve", tmpD, pn[:, 0:128], lg[:, h:h + 1], None, ALU.mult)
            stt("dve", tmpD, pn[:, 128:256], lg[:, 4 + h:5 + h], tmpD, ALU.mult, ALU.add)
            act(DT[:, h, :], tmpD, ACTF.Exp)
        for (ti, ki) in enumerate([0, 1, 4, 5]):
            src = kq[:, 4 * ki:4 * ki + 4].unsqueeze(2).to_broadcast([128, 4, 128])
            S.op("dve", lambda e, ti=ti, src=src: e.tensor_copy(out=KT[:, ti, :, :], in_=src),
                 [kq[:, 4 * ki:4 * ki + 4]], [KT[:, ti, :, :]])
        for k8 in range(8):
            ts("dve", GI[:, k8, :], identf, gC[:, k8:k8 + 1], None, ALU.mult)
        for h in range(4):
            ts("dve", QM[:, h, 0:128], identf, QK_SCALE, None, ALU.mult)
            ts("dve", QM[:, h, 128:256], identf, kq[:, 8 + h:9 + h], QK_SCALE, ALU.mult, ALU.mult)
            ts("dve", QM[:, h, 256:384], identf, kq[:, 12 + h:13 + h], QK_SCALE, ALU.mult, ALU.mult)

    def adaln_part():
        act(siluT, cTs, ACTF.Silu)
        for jc in range(16):
            mm([(ps[4][:, jc * 3:jc * 3 + 3], WADA[:, ic, jc * 128:(jc + 1) * 128], siluT[:, ic * 3:ic * 3 + 3],
                 ic == 0, ic == 7) for ic in range(8)])
        cp("dve", adaT, ps[4][:, 0:48])
        ada3 = adaT.rearrange("p (j r) -> p j r", j=16)
        tt("dve", shT, ada3[:, 0:8, :], bT[:, 0:8].unsqueeze(2).to_broadcast([128, 8, 3]), ALU.add)
        tt("dve", aT, ada3[:, 8:16, :], bT[:, 8:16].unsqueeze(2).to_broadcast([128, 8, 3]), ALU.add)
        stt("dve", aT, aT, 1.0, gpre.unsqueeze(2).to_broadcast([128, 8, 3]), ALU.add, ALU.mult)

    def gate_part():
        S.dma("pool", "wadg", [(WADG[:, k, :], wada_v[:, k, 2048:3072]) for k in range(8)], writes=[WADG[:, :, :]])
        S.dma("sp", "c2", [(bg_rep, bgate_d.partition_broadcast(128)), (gp_rep, gpost_d.partition_broadcast(128))],
              writes=[bg_rep, gp_rep])
        srep = v16(P0 + 28672, 8 * 2 * 128).rearrange("p (k b m) -> p k b m", k=8, b=2)
        sil3 = siluT.rearrange("p (k r) -> p k r", k=8)
        for b in range(NB):
            src = sil3[:, :, b:b + 1].to_broadcast([128, 8, 128])
            S.op("dve", lambda e, b=b, src=src: e.tensor_copy(out=srep[:, :, b, :], in_=src), [siluT], [srep[:, :, b, :]])
        ggt = v32(P0 + 24576, 1024)
        for b in range(NB):
            for half in range(2):
                bank = ps[1 + b * 2 + half]
                mm([(bank[:, :], srep[:, ic, b, :], WADG[:, ic, half * 512:half * 512 + 512],
                     ic == 0, ic == 7) for ic in range(8)])
                gsl = ggt[:, half * 512:(half + 1) * 512]
                tt("dve", gsl, bank[:, :], bg_rep[:, half * 512:(half + 1) * 512], ALU.add)
                tt("dve", gsl, gsl, gp_rep[:, half * 512:(half + 1) * 512], ALU.mult)
            o_ = S.dma("sp", "ggw", [(gg_scr[b, :, :], ggt)], reads=[ggt])
            GGTOK[b] = o_["tok"] if o_ is not None else None

    if lvl >= 3:
        for g in range(4):
            mm([(ps[5 + g // 2][:, (g % 2) * 256:(g % 2) * 256 + 128], CSbf[:, 0:128], wfbf[:, g, :], True, True),
                (ps[5 + g // 2][:, (g % 2) * 256 + 128:(g % 2) * 256 + 256], CSbf[:, 128:256], wfbf[:, g, :], True, True)])
        for g in range(4):
            cp("act", ABm[:, g, :, :], ps[5 + g // 2][:, (g % 2) * 256:(g % 2) * 256 + 256].rearrange("p (s d) -> p s d", s=2))

    def norm_transpose(src_tiles, row, dst, ntile, PA, GS=4, banks=None, tag="xt", ring=2, hook=None):
        o_xt = [PA.take(4096) for _ in range(ring * GS)]
        o_xn = [PA.take(2048) for _ in range(2)]
        o_junk = PA.take(2048) if ring == 2 else 0
        o_ssq2 = [PA.take(16) for _ in range(2)]
        o_var = PA.take(16)
        o_tA = PA.take(16)
        o_tB = PA.take(16)
        o_rs2 = [PA.take(16) for _ in range(2)]
        groups = list(range(0, ntile, GS))

        def ctxg(g0):
            gi_ = g0 // GS
            nt = min(GS, ntile - g0)
            bo = 4 * (gi_ % 2) if banks is None else banks
            return gi_, nt, bo, o_ssq2[gi_ % 2], o_rs2[gi_ % 2], GS * (gi_ % ring)

        def st1(g0):
            gi_, nt, bo, o_ssq, o_rs, so = ctxg(g0)
            ssq = v32(o_ssq, GS)
            S.op("dve", lambda e, ssq=ssq: e.memset(ssq, 0.0), [], [ssq])
            for t in range(nt):
                xt = v32(o_xt[so + t], 1024)
                o_ = S.dma("sp", "%s%d" % (tag, so + t), [(xt, src_tiles[g0 + t])], writes=[xt])
                if o_ is not None:
                    LASTX[0] = o_["tok"]
                act(v16(o_xn[t % 2], 1024) if ring == 1 else v16(o_junk, 1024), xt, ACTF.Square, accum_out=ssq[:, t:t + 1])
            ts("dve", v32(o_var, GS), ssq, 1.0 / D, EPS, ALU.mult, ALU.add)
            rsqrt_chain(o_var, v32(o_rs, GS), o_tA, o_tB, GS)

        def st2(g0):
            gi_, nt, bo, o_ssq, o_rs, so = ctxg(g0)
            rs = v32(o_rs, GS)
            for t in range(nt):
                xt = v32(o_xt[so + t], 1024)
                xn = v16(o_xn[t % 2], 1024)
                if t % 2 == 0:
                    act(xn, xt, ACTF.Copy, scale=rs[:, t:t + 1])
                else:
                    ts("dve", xn, xt, rs[:, t:t + 1], None, ALU.mult)
                tr([(psb[bo + ic // 2][:, (ic % 2) * 512 + t * 128:(ic % 2) * 512 + (t + 1) * 128],
                     xn[:, ic * 128:(ic + 1) * 128]) for ic in range(8)])

        def st3(g0):
            gi_, nt, bo, o_ssq, o_rs, so = ctxg(g0)
            for ic in range(8):
                src = psb[bo + ic // 2][:, (ic % 2) * 512:(ic % 2) * 512 + nt * 128]
                dstv = dst[:, ic, g0 * 128:(g0 + nt) * 128]
                if (ic // 2) % 2 == 0:
                    act(dstv, src, ACTF.Identity, scale=aT[:, ic, row:row + 1], bias=shT[:, ic, row:row + 1])
                else:
                    ts("dve", dstv, src, aT[:, ic, row:row + 1], shT[:, ic, row:row + 1], ALU.mult, ALU.add)

        if ring == 2 and len(groups) > 1:
            st1(groups[0])
            for k, g0 in enumerate(groups):
                st2(g0)
                if k + 1 < len(groups):
                    st1(groups[k + 1])
                if k == 0 and hook is not None:
                    hook()
                st3(g0)
                yield g0
        else:
            for g0 in groups:
                st1(g0)
                st2(g0)
                st3(g0)
                yield g0

    def ctx_phase(b, base=None):
        PA = Arena(P0 + 37248 if base is None else base)
        o_hTc = PA.take(8 * 256 * 2)
        hTc = v16(o_hTc, 2048).rearrange("p (k n) -> p k n", k=8)
        o_kf = [PA.take(1024) for _ in range(2)]
        o_kb = [PA.take(1024) for _ in range(2)]
        o_v = [PA.take(1024) for _ in range(2)]
        for _ in norm_transpose([ctx_d[b, t * 128:(t + 1) * 128, :] for t in range(2)], 2, hTc, 2, PA, GS=2, ring=1):
            yield
        for t in range(2):
            kb_, vb_ = ps[4 + 2 * t], ps[5 + 2 * t]
            items = []
            for h in range(4):
                for ic in range(8):
                    items.append((kb_[:, h * 128:(h + 1) * 128], hTc[:, ic, t * 128:(t + 1) * 128],
                                  W(ic, 1024 + h * 384 + 128, 1024 + h * 384 + 256), ic == 0, ic == 7))
                for ic in range(8):
                    items.append((vb_[:, h * 128:(h + 1) * 128], hTc[:, ic, t * 128:(t + 1) * 128],
                                  W(ic, 1024 + h * 384 + 256, 1024 + h * 384 + 384), ic == 0, ic == 7))
            mm(items)
            kf, kbv, vv = v16(o_kf[t], 512), v16(o_kb[t], 512), v16(o_v[t], 512)
            tf = KT[:, 2, :, :] if t == 0 else KT[:, 0, :, :]
            tb = KT[:, 1, :, :] if t == 0 else KT[:, 3, :, :]
            tt("dve", kf.rearrange("p (h d) -> p h d", h=4), kb_[:, :].rearrange("p (h d) -> p h d", h=4), tf, ALU.mult)
            tt("dve", kbv.rearrange("p (h d) -> p h d", h=4), kb_[:, :].rearrange("p (h d) -> p h d", h=4), tb, ALU.mult)
            cp("act", vv, vb_[:, :])
            yield
        for s in range(2):
            items = []
            for h in range(4):
                for t in range(2):
                    kk = v16(o_kf[t] if s == 0 else o_kb[t], 512)
                    items.append((ps[s][:, h * 128:(h + 1) * 128], kk[:, h * 128:(h + 1) * 128],
                                  v16(o_v[t], 512)[:, h * 128:(h + 1) * 128], t == 0, t == 1))
            mm(items)
            cp("act", S0b[b][:, s, :, :], ps[s][:, :].rearrange("p (h e) -> p h e", h=4))
            yield

    A_BASE = P0 + 41408

    def phase_A(b, GS=4, banks=None, base=None, ring=2, hook=None):
        PA = Arena(P0 if base is None else base)
        return norm_transpose([x_d[b, t * 128:(t + 1) * 128, :] for t in range(NT)], b, hT, NT, PA, GS=GS,
                              banks=banks, tag="xa", ring=ring, hook=hook)

    o_R1 = TOT - 8192
    R1 = v16(o_R1, 4096).rearrange("p (r c) -> p r c", r=16)

    def load_R1():
        S.dma("pool", "r1", [(R1, cd["c_R1"].rearrange("p (r c) -> p r c", r=16))], writes=[R1])

    def fourier_phase(b):
        PA = Arena(P0)
        o_U = [PA.take(512) for _ in range(4)]
        o_Tt = PA.take(2 * 2 * 2048 * 2)
        Tt = v16(o_Tt, 8192).rearrange("p (g s j r) -> p g s j r", g=2, s=2, j=128)
        o_Ttok = PA.take(16 * 2 * 128 * 2)
        Ttok = v16(o_Ttok, 4096).rearrange("p (j s c) -> p j s c", j=16, s=2)
        o_Yt = PA.take(2 * 2048 * 2)
        Yt = v16(o_Yt, 4096).rearrange("p (s l) -> p s l", s=2)
        o_sf = [PA.take(4096) for _ in range(2)]
        sfTs = [v16(o_sf[0], 2048), v16(o_sf[1], 2048)]
        for hf in range(2):
            def fproj(r, hf=hf):
                U = v16(o_U[r % 4], 256)
                pu = ps[r % 2]
                mm([(pu[:, 0:256], hT[:, ic, r:2048:16], W(ic, hf * 256, hf * 256 + 256), ic == 0, ic == 7)
                    for ic in range(8)])
                cp("act", U, pu[:, 0:256])

            def stage1pair(p):
                r0 = 2 * p
                for gi in range(2):
                    bk = ps[2 + (p % 2) * 2 + gi]
                    mm([(bk[:, rl * 256:(rl + 1) * 256], v16(o_U[(r0 + rl) % 4], 256)[:, gi * 128:(gi + 1) * 128],
                         R1[:, r0 + rl, :], True, True) for rl in range(2)])
                    src = bk[:, :].rearrange("p (r s j) -> p s j r", r=2, s=2)
                    dstv = Tt[:, gi, :, :, r0:r0 + 2]
                    if (p + gi) % 2 == 0:
                        S.op("dve", lambda e, dstv=dstv, src=src: e.tensor_copy(out=dstv, in_=src), [src], [dstv])
                    else:
                        cp("act", dstv, src)

            def gate_chunk(k, hf=hf):
                gi, tg = k // 4, k % 4
                g = hf * 2 + gi
                pg = ps[6 + k % 2]
                mm([(pg[:, :], W(ic, 512 + g * 128, 512 + (g + 1) * 128), hT[:, ic, tg * 512:(tg + 1) * 512],
                     ic == 0, ic == 7) for ic in range(8)])
                act(sfTs[gi][:, tg * 512:(tg + 1) * 512], pg[:, :], ACTF.Silu)
            fproj(0)
            fproj(1)
            for p in range(8):
                if 2 * p + 2 < 16:
                    fproj(2 * p + 2)
                    fproj(2 * p + 3)
                stage1pair(p)
                gate_chunk(p)
            for gi in range(2):
                g = hf * 2 + gi
                sfT = sfTs[gi]
                for q4 in range(4):
                    pb = psb[4 + q4]
                    tr([(pb[:, (jj * 2 + s) * 128:(jj * 2 + s + 1) * 128],
                         Tt[:, gi, s, (q4 * 4 + jj) * 8:(q4 * 4 + jj + 1) * 8, :].rearrange("p j r -> p (j r)"))
                        for jj in range(4) for s in range(2)])
                    dstv = Ttok[:, q4 * 4:(q4 + 1) * 4, :, :]
                    src = pb[:, :].rearrange("p (j s c) -> p j s c", j=4, s=2)
                    if q4 % 2 == 0:
                        cp("act", dstv, src)
                    else:
                        S.op("dve", lambda e, dstv=dstv, src=src: e.tensor_copy(out=dstv, in_=src), [src], [dstv])
                for jh in range(16):
                    bk = jh % 8
                    pz = ps[bk]
                    mm([(pz[:, 0:256], Ttok[:, jh, 0, :], Mbf[:, 0:256], True, False),
                        (pz[:, 0:256], Ttok[:, jh, 1, :], Mbf[:, 256:512], False, True)])
                    src = pz[:, 0:256].rearrange("p (s j r) -> p s r j", s=2, j=8)
                    dstv = Yt[:, :, :].rearrange("p s (r j) -> p s r j", r=16)[:, :, :, 8 * jh:8 * jh + 8]
                    if bk % 2 == 0:
                        S.op("dve", lambda e, dstv=dstv, src=src: e.tensor_copy(out=dstv, in_=src), [src], [dstv])
                    else:
                        cp("act", dstv, src)
                for lgp in range(4):
                    pg = ps[4 + lgp]
                    mm([(pg[:, :], ABm[:, g, 0, :], Yt[:, 0, lgp * 512:(lgp + 1) * 512], True, False),
                        (pg[:, :], ABm[:, g, 1, :], Yt[:, 1, lgp * 512:(lgp + 1) * 512], False, True)])
                    tt("dve", mixT[:, g, lgp * 512:(lgp + 1) * 512], pg[:, :], sfT[:, lgp * 512:(lgp + 1) * 512], ALU.mult)

    def retention_phase(b):
        PA = Arena(P0)
        o_qkv0 = PA.take(16 * 384 * 2)
        qkvs = [v16(o_qkv0, 16 * 384).rearrange("p (t c) -> p t c", t=16),
                v16(o_WINF, 16 * 384).rearrange("p (t c) -> p t c", t=16)]
        o_t1 = PA.take(4096)
        o_t2 = PA.take(4096)
        o_Sf = PA.take(4096)
        o_Sb = PA.take(4096)
        Sf = v16(o_Sf, 2048).rearrange("p (c e) -> p c e", c=16)
        Sb = v16(o_Sb, 2048).rearrange("p (c e) -> p c e", c=16)
        o_sg0 = PA.take(4096)
        sgTs = [v16(o_sg0, 2048), v16(o_WINF + 12288, 2048)]
        o_st = [PA.take(512) for _ in range(4)]
        o_q3 = PA.take(16 * 384 * 2)
        q3all = v16(o_q3, 16 * 384).rearrange("p (c n) -> p c n", c=16)
        o_kT = PA.take(4096)
        kTall = v16(o_kT, 2048).rearrange("p (c n) -> p c n", c=16)
        o_z = [PA.take(256) for _ in range(2)]
        o_bn = PA.take(16 * 6 * 4)
        o_var = PA.take(64)
        o_tA = PA.take(64)
        o_tB = PA.take(64)
        o_rs = PA.take(64)
        bn = v32(o_bn, 96).rearrange("p (c s) -> p c s", c=16)

        def proj(h):
            qkv, sgT = qkvs[h % 2], sgTs[h % 2]
            for t in range(16):
                pq = ps[t % 4] if t < 8 else ps[t % 2]
                mm([(pq[:, 0:384], hT[:, ic, t * 128:(t + 1) * 128], W(ic, 1024 + h * 384, 1024 + (h + 1) * 384),
                     ic == 0, ic == 7) for ic in range(8)])
                cp("act", qkv[:, t, :], pq[:, 0:384])
                yield
            for tg in range(4):
                pg = ps[tg % 2]
                mm([(pg[:, :], W(ic, 2560 + h * 128, 2560 + (h + 1) * 128), hT[:, ic, tg * 512:(tg + 1) * 512],
                     ic == 0, ic == 7) for ic in range(8)])
                act(sgT[:, tg * 512:(tg + 1) * 512], pg[:, :], ACTF.Silu)
                yield

        o_O = PA.take(4096)
        O = v16(o_O, 2048).rearrange("p (c e) -> p c e", c=16)
        o_mean = PA.take(64)
        o_nb = PA.take(64)
        mean16, nb16 = v32(o_mean, 16), v32(o_nb, 16)

        def tail_rot(h):
            qkv = qkvs[h % 2]
            for hf in range(2):
                tsl = slice(8 * hf, 8 * hf + 8)
                X = qkv[:, tsl, 0:256].rearrange("p t (a d) -> p t a d", a=2)
                t1 = v16(o_t1, 2048).rearrange("p (t a d) -> p t a d", t=8, a=2)
                t2 = v16(o_t2, 2048).rearrange("p (t a d) -> p t a d", t=8, a=2)
                for a in range(2):
                    tt("dve", t1[:, :, a, :], X[:, :, a, :], COSt[:, tsl, :], ALU.mult)
                    Xa = X[:, :, a, :].rearrange("p t (f u j) -> p t f u j", f=2, u=2)
                    t2a = t2[:, :, a, :].rearrange("p t (f u j) -> p t f u j", f=2, u=2)
                    Sa = SINt[:, tsl, :].rearrange("p t (f u j) -> p t f u j", f=2, u=2)
                    tt("dve", t2a[:, :, :, 0, :], Xa[:, :, :, 1, :], Sa[:, :, :, 0, :], ALU.mult)
                    tt("dve", t2a[:, :, :, 1, :], Xa[:, :, :, 0, :], Sa[:, :, :, 1, :], ALU.mult)
                for a in range(2):
                    tt("dve", X[:, :, a, :], t1[:, :, a, :], t2[:, :, a, :], ALU.add)
                yield
            KF = v16(o_t1, 2048).rearrange("p (c d) -> p c d", c=16)
            KB = v16(o_t2, 2048).rearrange("p (c d) -> p c d", c=16)
            kview = qkv[:, :, 128:256]
            tt("dve", KF, kview, KT[:, 0, h:h + 1, :].to_broadcast([128, 16, 128]), ALU.mult)
            tt("dve", KB, kview, KT[:, 1, h:h + 1, :].to_broadcast([128, 16, 128]), ALU.mult)
            S0 = S0b[b]
            cp("dve", v32(o_st[0], 128), S0[:, 0, h, :])
            cp("dve", v32(o_st[2], 128), S0[:, 1, h, :])
            cp("act", Sf[:, 0, :], S0[:, 0, h, :])
            cp("act", Sb[:, 15, :], S0[:, 1, h, :])
            yield

        def tail_l1(h):
            qkv = qkvs[h % 2]
            KF = v16(o_t1, 2048).rearrange("p (c d) -> p c d", c=16)
            KB = v16(o_t2, 2048).rearrange("p (c d) -> p c d", c=16)
            stF = [v32(o_st[0], 128), v32(o_st[1], 128)]
            stB = [v32(o_st[2], 128), v32(o_st[3], 128)]
            for c in range(16):
                pf = ps[2 + c % 3]
                pfb = psb[2 + c % 3]
                mm([(pf[:, 0:384], qkv[:, c, 0:128], QM[:, h, :], True, True)])
                tr([(pfb[:, 768:896], qkv[:, c, 128:256])])
                cp("act", q3all[:, c, :], pf[:, 0:384])
                cp("act", kTall[:, c, :], pfb[:, 768:896])
                if c < 15:
                    pa = ps[5 + c % 3]
                    cb = 15 - c
                    cur, nxt_ = c % 2, 1 - c % 2
                    mm([(pa[:, 0:128], KF[:, c, :], qkv[:, c, 256:384], True, True),
                        (pa[:, 128:256], KB[:, cb, :], qkv[:, cb, 256:384], True, True)])
                    stt("dve", stF[nxt_], stF[cur], gC[:, h:h + 1], pa[:, 0:128], ALU.mult, ALU.add)
                    stt("dve", stB[nxt_], stB[cur], gC[:, 4 + h:5 + h], pa[:, 128:256], ALU.mult, ALU.add)
                    cp("pool", Sf[:, c + 1, :], stF[nxt_])
                    cp("pool", Sb[:, cb - 1, :], stB[nxt_])
                yield

        def tail_l2(h):
            qkv = qkvs[h % 2]
            PTall = v16(o_t2, 2048).rearrange("p (c n) -> p c n", c=16)

            def outs(c):
                pO = ps[4 + c % 4]
                mm([(pO[:, 0:128], PTall[:, c, :], qkv[:, c, 256:384], True, False),
                    (pO[:, 0:128], q3all[:, c, 128:256], Sf[:, c, :], False, False),
                    (pO[:, 0:128], q3all[:, c, 256:384], Sb[:, c, :], False, True)])
                S.op("dve", lambda e, c=c, pO=pO: e.bn_stats(out=bn[:, c, :], in_=pO[:, 0:128]), [pO[:, 0:128]], [bn[:, c, :]])
                cp("act", O[:, c, :], pO[:, 0:128])
            for cg in range(5):
                if cg < 4:
                    psS = ps[2 + cg % 2]
                    mm([(psS[:, cc * 128:(cc + 1) * 128], kTall[:, cg * 4 + cc, :], q3all[:, cg * 4 + cc, 0:128], True, True)
                        for cc in range(4)])
                    tt("dve", PTall[:, cg * 4:(cg + 1) * 4, :], psS[:, :].rearrange("p (c n) -> p c n", c=4),
                       DT[:, h:h + 1, :].to_broadcast([128, 4, 128]), ALU.mult)
                    yield
                if cg >= 1:
                    for cc in range(4):
                        outs((cg - 1) * 4 + cc)
                        yield

        def tail_i(h):
            sgT = sgTs[h % 2]
            tA, tB = v32(o_tA, 16), v32(o_tB, 16)
            var = v32(o_var, 16)
            tt("dve", mean16, bn[:, :, 1], bn[:, :, 4], ALU.add)
            tt("dve", tA, bn[:, :, 1], bn[:, :, 4], ALU.subtract)
            tt("dve", tA, tA, tA, ALU.mult)
            tt("dve", tB, bn[:, :, 2], bn[:, :, 5], ALU.add)
            ts("dve", tB, tB, 1.0 / 128, EPS, ALU.mult, ALU.add)
            stt("dve", var, tA, 0.25, tB, ALU.mult, ALU.add)
            ts("dve", mean16, mean16, 0.5, None, ALU.mult)
            rs = v32(o_rs, 16)
            rsqrt_chain(o_var, rs, o_tA, o_tB, 16)
            stt("dve", nb16, mean16, -1.0, rs, ALU.mult, ALU.mult)
            for c in range(16):
                if c % 2 == 0:
                    ts("pool", O[:, c, :], O[:, c, :], rs[:, c:c + 1], nb16[:, c:c + 1], ALU.mult, ALU.add)
                else:
                    act(O[:, c, :], O[:, c, :], ACTF.Identity, scale=rs[:, c:c + 1], bias=nb16[:, c:c + 1])
            yield
            for cg in range(4):
                pz = psb[cg % 2]
                tr([(pz[:, cc * 128:(cc + 1) * 128], O[:, cg * 4 + cc, :]) for cc in range(4)])
                stt("dve", mixT[:, 4 + h, cg * 512:(cg + 1) * 512], pz[:, 0:512], gn[:, h:h + 1],
                    sgT[:, cg * 512:(cg + 1) * 512], ALU.mult, ALU.mult)
                yield

        def adv(g, n):
            for _ in range(n):
                next(g, None)

        for _ in proj(0):
            pass
        for h in range(4):
            nxt = proj(h + 1) if h + 1 < 4 else iter(())
            if h == 3:
                load_WOUT()
            for j_, _ in enumerate(tail_rot(h)):
                adv(nxt, (3, 3, 2)[j_])
            ig = tail_i(h - 1) if h >= 1 else iter(())
            for k, _ in enumerate(tail_l1(h)):
                if k in (0, 12, 13, 14, 15):
                    next(ig, None)
                if k in (1, 3, 5, 7, 9, 11):
                    adv(nxt, 1)
            for _ in ig:
                pass
            for _ in tail_l2(h):
                adv(nxt, 1)
            for _ in nxt:
                pass
        for _ in tail_i(3):
            pass

    WOUT = v16(o_BIG, 8192).rearrange("p (k c) -> p k c", k=8)

    def load_WOUT():
        S.dma("pool", "wout", [(WOUT, wout_d.rearrange("(k p) c -> p k c", p=128))], writes=[WOUT])

    XCOPY = {}

    def xcopy(b):
        o = S.dma("sp", "xcopy%d" % b, [(out_d[b, :, :], x_d[b, :, :])])
        XCOPY[b] = o["tok"] if o is not None else None

    def out_phase(b):
        if b + 1 < nb:
            load_WINF()
        PA = Arena(P0)
        o_gg = PA.take(4096)
        GGb = v32(o_gg, 1024)
        S.dma("sp", "ggr", [(GGb, gg_scr[b, :, :])], writes=[GGb], deps=[GGTOK[b]] if GGTOK.get(b) else [])
        o_y = [PA.take(4096) for _ in range(8)]
        o_ss = [PA.take(32) for _ in range(2)]
        o_var = PA.take(16)
        o_tA = PA.take(16)
        o_tB = PA.take(16)
        o_rs = [PA.take(16) for _ in range(2)]
        for g0 in range(0, NT, 4):
            gp = (g0 // 4) % 2
            ss = v32(o_ss[gp], 8)
            S.op("dve", lambda e, ss=ss: e.memset(ss, 0.0), [], [ss])
            for t in range(4):
                tile_i = g0 + t
                y = v32(o_y[tile_i % 8], 1024)
                for half in range(2):
                    py = ps[(t % 4) * 2 + half]
                    mm([(py[:, :], mixT[:, kc, tile_i * 128:(tile_i + 1) * 128], WOUT[:, kc, half * 512:(half + 1) * 512],
                         kc == 0, kc == 7) for kc in range(8)])
                    cp("act", y[:, half * 512:(half + 1) * 512], py[:, :])
                    act(py[:, :], py[:, :], ACTF.Square, accum_out=ss[:, t * 2 + half:t * 2 + half + 1])
            ss2 = ss.rearrange("p (t h) -> p t h", h=2)
            var = v32(o_var, 4)
            tt("dve", var, ss2[:, :, 0], ss2[:, :, 1], ALU.add)
            ts("dve", var, var, 1.0 / D, EPS, ALU.mult, ALU.add)
            rs = v32(o_rs[gp], 4)
            rsqrt_chain(o_var, rs, o_tA, o_tB, 4)
            for t in range(4):
                tile_i = g0 + t
                y = v32(o_y[tile_i % 8], 1024)
                stt("dve", y, y, rs[:, t:t + 1], GGb, ALU.mult, ALU.mult)
                S.dma("pool", "oy%d" % (tile_i % 8), [(out_d[b, tile_i * 128:(tile_i + 1) * 128, :], y)], reads=[y],
                      deps=[XCOPY[b]] if XCOPY.get(b) else [], kw=dict(accum_op=ALU.add))
            yield g0

    def drain(gen):
        for _ in gen:
            pass

    load_R1()
    if "A" in stages:
        drain(phase_A(0, hook=adaln_part))
    else:
        adaln_part()
    load_WINR()
    for b in range(nb):
        if "F" in stages:
            fourier_phase(b)
        if b == 0:
            if "ctx" in stages:
                drain(ctx_phase(0))
            if lvl >= 3:
                gate_part()
        if "O" in stages:
            xcopy(b)
        if "R" in stages:
            retention_phase(b)
        if "O" in stages:
            drain(out_phase(b))
        if b + 1 < nb and "A" in stages:
            ag = phase_A(b + 1, base=P0 + 37504, GS=4, ring=1)
            cg = ctx_phase(b + 1, base=P0) if "ctx" in stages else iter(())
            for _ in ag:
                next(cg, None)
            drain(cg)
            load_R1()

    DBG.update(dict(hT=hT.rearrange("p k n -> p (k n)"),
                    mixT=mixT.rearrange("p k n -> p (k n)"), aT=v32(o_aT, 24), shT=v32(o_shT, 24),
                    S0=v32(o_S0, 1024), DT=v16(o_DT, 512), QM=v16(o_QM, 1536),
                    KT=v16(o_KT, 2048), ABm=v16(o_AB, 1024), lg=lg, gC=gC, kq=kq, adaT=adaT,
                    siluT=siluT, ident=ident, COSt=v16(o_COS, 2048), PH=v16(P0, (TOT - P0) // 2)))
    for i, name in enumerate(dumps):
        apv = DBG[name]
        dd = nc.dram_tensor("dbg_" + name, [128, int(apv.shape[1])], apv.dtype, kind="ExternalOutput").ap()
        S.dma("sp", "dbg%d" % i, [(dd, apv)], reads=[apv], force=True)

    print("nrec", S.nrec, {e: len(v) for e, v in S.eng_ops.items()})
    S.emit(st)
    st.close()
    return nc


_CACHE = {}


def kernel(x, c, ctx, c_ctx, w_ada, b_ada, g_pre, g_post, w_in, w_fourier, decay_logit, ret_gn_gain, w_out):
    f = lambda a: np.ascontiguousarray(np.asarray(a, dtype=np.float32))
    x, c, ctx, c_ctx = f(x), f(c), f(ctx), f(c_ctx)
    w_ada, b_ada, g_pre, g_post = f(w_ada)[0], f(b_ada)[0], f(g_pre)[0], f(g_post)[0]
    w_in, w_fourier, decay_logit = f(w_in)[0], f(w_fourier)[0], f(decay_logit)[0]
    ret_gn_gain, w_out = f(ret_gn_gain)[0], f(w_out)[0]
    if "nc" not in _CACHE:
        _CACHE["nc"] = build_program()
        _CACHE["consts"] = host_consts()
    nc = _CACHE["nc"]
    consts = _CACHE["consts"]
    shared = {
        "w_ada": w_ada, "w_in": w_in, "w_out": w_out, "w_f": w_fourier,
        "bT": f(b_ada.reshape(24, 128).T), "b_gate": f(b_ada[2048:3072]),
        "g_preT": f(g_pre.reshape(8, 128).T), "g_post": g_post,
        "dlog": f(decay_logit.reshape(8)), "gnT": f(ret_gn_gain.reshape(4, 128).T),
    }
    shared.update(consts)
    in_maps = []
    for i in range(N_CORES):
        rows = np.stack([c[NB * i], c[NB * i + 1], c_ctx], axis=0)
        cT = f(rows.reshape(3, 8, 128).transpose(2, 1, 0).reshape(128, 24))
        m = dict(shared)
        m["x"] = f(x[NB * i:NB * (i + 1)])
        m["ctx"] = f(ctx[NB * i:NB * (i + 1)])
        m["cT"] = cT
        in_maps.append(m)
    res = run_bass_kernel_spmd(nc, in_maps, core_ids=list(range(N_CORES)))
    return np.concatenate([np.asarray(r["out"], dtype=np.float32) for r in res.results], axis=0)
```
